# Optimizing a Trainium2 kernel written in Bass

```python
import math
import jax, jax.numpy as jnp
from jax import lax
import numpy as np

D_MODEL = 2048
BATCH = 2
SEQ = 4096
DEPTH = 2

N_META = 16
CHUNK = 128
SUB = 16
EPS = 1e-6

RET_HEADS = 8
RET_DK = 128
RET_DV = 256
RET_QK = RET_HEADS * RET_DK
RET_W = RET_HEADS * RET_DV
ROPE_BASE = 10000.0

S5_W = 1024
S5_GH = 16
S5_G = S5_W // S5_GH
S5_P = 64
DT_MIN = 1e-3
DT_MAX = 1e-1

GLA_HEADS = 4
GLA_DK = 256
GLA_DV = 512
GLA_QK = GLA_HEADS * GLA_DK
GLA_W = GLA_HEADS * GLA_DV
GLA_RANK = 16
GLA_TAU = 16.0

IN_AB = 2 * RET_QK + 2 * RET_W + 2 * S5_W
OUT_AB = RET_W + S5_W
IN_C = 2 * GLA_QK + 2 * GLA_W + GLA_RANK
N_EVEN = (DEPTH + 1) // 2
N_ODD = DEPTH // 2

kernel_name = "hybrid_retention_s5_gla_meta"


def _rmsnorm(x, w):
    xf = x.astype(jnp.float32)
    y = xf * lax.rsqrt(jnp.mean(xf * xf, axis=-1, keepdims=True) + EPS)
    return (y * w.astype(jnp.float32)).astype(x.dtype)


def _head_rmsnorm(o, w):
    y = o * lax.rsqrt(jnp.mean(o * o, axis=-1, keepdims=True) + EPS)
    b, l, hh, d = o.shape
    return y.reshape(b, l, hh * d) * w.astype(jnp.float32)


def _rope(t, cos, sin):
    half = t.shape[-1] // 2
    t1, t2 = t[..., :half], t[..., half:]
    c, s = cos[None, :, None, :], sin[None, :, None, :]
    return jnp.concatenate([t1 * c - t2 * s, t1 * s + t2 * c], axis=-1)


def _to_chunks(t):
    t = jnp.pad(t, ((0, 0), (CHUNK - N_META, 0), (0, 0), (0, 0)))
    b, lp, hh, d = t.shape
    return t.reshape(b, lp // CHUNK, CHUNK, hh, d).transpose(0, 3, 1, 2, 4)


def _from_chunks(t):
    b, hh, n, c, d = t.shape
    return t.transpose(0, 2, 3, 1, 4).reshape(b, n * c, hh, d)[:, CHUNK - N_META:]


def _retention(q, k, v):
    qc, kc, vc = _to_chunks(q), _to_chunks(k), _to_chunks(v)
    bsz = qc.shape[0]
    log_g = jnp.log1p(-jnp.exp2(-5.0 - jnp.arange(RET_HEADS, dtype=jnp.float32)))
    idx = jnp.arange(CHUNK, dtype=jnp.float32)
    diff = idx[:, None] - idx[None, :]
    causal = diff >= 0
    decay = jnp.where(causal, jnp.exp(log_g[:, None, None] * jnp.maximum(diff, 0.0)), 0.0)
    scores = jnp.einsum('bhnid,bhnjd->bhnij', qc, kc) * decay[None, :, None]
    o_intra = jnp.einsum('bhnij,bhnjv->bhniv', scores, vc)
    k_w = kc * jnp.exp(log_g[:, None] * (CHUNK - 1 - idx))[None, :, None, :, None]
    kv = jnp.einsum('bhnjd,bhnjv->nbhdv', k_w, vc)
    g_chunk = jnp.exp(log_g * CHUNK)[None, :, None, None]

    def step(s, kv_n):
        return s * g_chunk + kv_n, s

    s0 = jnp.zeros((bsz, RET_HEADS, RET_DK, RET_DV), jnp.float32)
    _, s_prev = lax.scan(step, s0, kv)
    q_w = qc * jnp.exp(log_g[:, None] * (idx + 1.0))[None, :, None, :, None]
    o_inter = jnp.einsum('bhnid,nbhdv->bhniv', q_w, s_prev)
    return _from_chunks(o_intra + o_inter)


def _s5(u, lam_re, lam_im, log_dt, b_re, b_im, c_re, c_im, d, w_glu):
    bsz, L, _ = u.shape
    lam_re = lam_re.astype(jnp.float32); lam_im = lam_im.astype(jnp.float32)
    dt = jnp.exp(log_dt.astype(jnp.float32))[:, None]
    mag = jnp.exp(lam_re * dt)
    ab_re, ab_im = mag * jnp.cos(lam_im * dt), mag * jnp.sin(lam_im * dt)
    den = lam_re * lam_re + lam_im * lam_im
    nr, ni = ab_re - 1.0, ab_im
    f_re = (nr * lam_re + ni * lam_im) / den
    f_im = (ni * lam_re - nr * lam_im) / den
    b_re = b_re.astype(jnp.float32); b_im = b_im.astype(jnp.float32)
    bb_re = f_re[..., None] * b_re - f_im[..., None] * b_im
    bb_im = f_re[..., None] * b_im + f_im[..., None] * b_re
    ug = u.reshape(bsz, L, S5_G, S5_GH)
    bu_re = jnp.einsum('blgh,gph->lbgp', ug, bb_re)
    bu_im = jnp.einsum('blgh,gph->lbgp', ug, bb_im)
    a_re = jnp.broadcast_to(ab_re, bu_re.shape)
    a_im = jnp.broadcast_to(ab_im, bu_im.shape)

    def combine(e1, e2):
        a1r, a1i, b1r, b1i = e1
        a2r, a2i, b2r, b2i = e2
        return (a2r * a1r - a2i * a1i,
                a2r * a1i + a2i * a1r,
                a2r * b1r - a2i * b1i + b2r,
                a2r * b1i + a2i * b1r + b2i)

    _, _, x_re, x_im = lax.associative_scan(combine, (a_re, a_im, bu_re, bu_im), axis=0)
    y = (jnp.einsum('lbgp,ghp->blgh', x_re, c_re.astype(jnp.float32))
         - jnp.einsum('lbgp,ghp->blgh', x_im, c_im.astype(jnp.float32)))
    y = y.reshape(bsz, L, S5_W) + d.astype(jnp.float32) * u
    y = jax.nn.gelu(y)
    return y * jax.nn.sigmoid(y @ w_glu.astype(jnp.float32))


def _gla(q, k, v, log_a):
    qc, kc, vc, gc = _to_chunks(q), _to_chunks(k), _to_chunks(v), _to_chunks(log_a)
    bsz, hh, n, c, dk = qc.shape
    dv = vc.shape[-1]
    nsub = CHUNK // SUB
    b = jnp.cumsum(gc, axis=3)
    b_last = b[:, :, :, -1]
    kv = jnp.einsum('bhnjd,bhnjv->nbhdv', kc * jnp.exp(b_last[:, :, :, None] - b), vc)
    dec = jnp.exp(b_last).transpose(2, 0, 1, 3)

    def step(s, inp):
        kv_n, dec_n = inp
        return s * dec_n[..., None] + kv_n, s

    s0 = jnp.zeros((bsz, hh, dk, dv), jnp.float32)
    _, s_prev = lax.scan(step, s0, (kv, dec))
    o_inter = jnp.einsum('bhnid,nbhdv->bhniv', qc * jnp.exp(b), s_prev)
    bs = b.reshape(bsz, hh, n, nsub, SUB, dk)
    qs = qc.reshape(bsz, hh, n, nsub, SUB, dk)
    ks = kc.reshape(bsz, hh, n, nsub, SUB, dk)
    vs = vc.reshape(bsz, hh, n, nsub, SUB, dv)
    b_ref = jnp.concatenate([jnp.zeros_like(bs[:, :, :, :1, 0]), bs[:, :, :, :-1, -1]], axis=3)
    q_hat = qs * jnp.exp(bs - b_ref[:, :, :, :, None])
    j_pos = jnp.arange(CHUNK)
    before = j_pos[None, :] < (jnp.arange(nsub) * SUB)[:, None]
    expo = jnp.where(before[:, :, None], b_ref[:, :, :, :, None] - b[:, :, :, None], -jnp.inf)
    k_hat = kc[:, :, :, None] * jnp.exp(expo)
    s_cross = jnp.einsum('bhnsid,bhnsjd->bhnsij', q_hat, k_hat)
    o_cross = jnp.einsum('bhnsij,bhnjv->bhnsiv', s_cross, vc)
    tri = jnp.arange(SUB)[:, None] >= jnp.arange(SUB)[None, :]
    expo_d = jnp.where(tri[:, :, None], bs[:, :, :, :, :, None] - bs[:, :, :, :, None], -jnp.inf)
    s_diag = jnp.einsum('bhnsid,bhnsjd,bhnsijd->bhnsij', qs, ks, jnp.exp(expo_d))
    o_diag = jnp.einsum('bhnsij,bhnsjv->bhnsiv', s_diag, vs)
    o_intra = (o_cross + o_diag).reshape(bsz, hh, n, c, dv)
    return _from_chunks(o_intra + o_inter)


def _mixer_ab(h, w_in, ret_norm_w, lam_re, lam_im, log_dt, b_re, b_im, c_re, c_im, d, w_glu, w_out, cos, sin):
    bsz, L, _ = h.shape
    proj = (h @ w_in).astype(jnp.float32)
    q, k, v, z_a, u, z_b = jnp.split(
        proj, [RET_QK, 2 * RET_QK, 2 * RET_QK + RET_W, 2 * RET_QK + 2 * RET_W,
               2 * RET_QK + 2 * RET_W + S5_W], axis=-1)
    q = _rope(q.reshape(bsz, L, RET_HEADS, RET_DK), cos, sin)
    k = _rope(k.reshape(bsz, L, RET_HEADS, RET_DK), cos, sin) * (RET_DK ** -0.5)
    v = v.reshape(bsz, L, RET_HEADS, RET_DV)
    o_a = _head_rmsnorm(_retention(q, k, v), ret_norm_w) * jax.nn.silu(z_a)
    o_b = _s5(u, lam_re, lam_im, log_dt, b_re, b_im, c_re, c_im, d, w_glu) * jax.nn.silu(z_b)
    return jnp.concatenate([o_a, o_b], axis=-1).astype(h.dtype) @ w_out


def _mixer_c(h, w_in, w_gate, b_gate, norm_w, w_out):
    bsz, L, _ = h.shape
    proj = (h @ w_in).astype(jnp.float32)
    q, k, v, z, g_low = jnp.split(
        proj, [GLA_QK, 2 * GLA_QK, 2 * GLA_QK + GLA_W, 2 * GLA_QK + 2 * GLA_W], axis=-1)
    log_a = jax.nn.log_sigmoid(g_low @ w_gate.astype(jnp.float32) + b_gate.astype(jnp.float32)) / GLA_TAU
    o = _gla(q.reshape(bsz, L, GLA_HEADS, GLA_DK) * (GLA_DK ** -0.5),
             k.reshape(bsz, L, GLA_HEADS, GLA_DK),
             v.reshape(bsz, L, GLA_HEADS, GLA_DV),
             log_a.reshape(bsz, L, GLA_HEADS, GLA_DK))
    o = _head_rmsnorm(o, norm_w) * jax.nn.silu(z)
    return o.astype(h.dtype) @ w_out


def setup_inputs(seed: int = 0) -> dict:
    key = jax.random.key(seed)
    ks = jax.random.split(key, 24)
    f32 = jnp.float32

    def nrm(k, shape, scale):
        return jax.random.normal(k, shape, f32) * scale

    return {
        "x": nrm(ks[0], (BATCH, SEQ, D_MODEL), 1.0),
        "meta": nrm(ks[1], (N_META, D_MODEL), 1.0),
        "norm_ab_w": 1.0 + nrm(ks[2], (N_EVEN, D_MODEL), 0.02),
        "w_in_ab": nrm(ks[3], (N_EVEN, D_MODEL, IN_AB), D_MODEL ** -0.5),
        "ret_norm_w": 1.0 + nrm(ks[4], (N_EVEN, RET_W), 0.02),
        "s5_lam_re": -0.5 + nrm(ks[5], (N_EVEN, S5_G, S5_P), 0.01),
        "s5_lam_im": math.pi * jnp.arange(S5_P, dtype=f32) + nrm(ks[6], (N_EVEN, S5_G, S5_P), 0.01),
        "s5_log_dt": jax.random.uniform(ks[7], (N_EVEN, S5_G), f32, math.log(DT_MIN), math.log(DT_MAX)),
        "s5_b_re": nrm(ks[8], (N_EVEN, S5_G, S5_P, S5_GH), (2 * S5_GH) ** -0.5),
        "s5_b_im": nrm(ks[9], (N_EVEN, S5_G, S5_P, S5_GH), (2 * S5_GH) ** -0.5),
        "s5_c_re": nrm(ks[10], (N_EVEN, S5_G, S5_GH, S5_P), S5_P ** -0.5),
        "s5_c_im": nrm(ks[11], (N_EVEN, S5_G, S5_GH, S5_P), S5_P ** -0.5),
        "s5_d": nrm(ks[12], (N_EVEN, S5_W), 1.0),
        "s5_w_glu": nrm(ks[13], (N_EVEN, S5_W, S5_W), S5_W ** -0.5),
        "w_out_ab": nrm(ks[14], (N_EVEN, OUT_AB, D_MODEL), OUT_AB ** -0.5),
        "norm_c_w": 1.0 + nrm(ks[15], (N_ODD, D_MODEL), 0.02),
        "w_in_c": nrm(ks[16], (N_ODD, D_MODEL, IN_C), D_MODEL ** -0.5),
        "gla_w_gate": nrm(ks[17], (N_ODD, GLA_RANK, GLA_QK), GLA_RANK ** -0.5),
        "gla_b_gate": nrm(ks[18], (N_ODD, GLA_QK), 0.1),
        "gla_norm_w": 1.0 + nrm(ks[19], (N_ODD, GLA_W), 0.02),
        "w_out_c": nrm(ks[20], (N_ODD, GLA_W, D_MODEL), GLA_W ** -0.5),
        "final_norm_w": 1.0 + nrm(ks[21], (D_MODEL,), 0.02),
    }


def reference(x, meta, norm_ab_w, w_in_ab, ret_norm_w, s5_lam_re, s5_lam_im, s5_log_dt,
              s5_b_re, s5_b_im, s5_c_re, s5_c_im, s5_d, s5_w_glu, w_out_ab,
              norm_c_w, w_in_c, gla_w_gate, gla_b_gate, gla_norm_w, w_out_c, final_norm_w):
    bsz = x.shape[0]
    h = jnp.concatenate(
        [jnp.broadcast_to(meta.astype(x.dtype)[None], (bsz, N_META, D_MODEL)), x], axis=1)
    L = h.shape[1]
    pos = jnp.arange(L, dtype=jnp.float32)
    inv_freq = jnp.power(ROPE_BASE, -jnp.arange(0, RET_DK, 2, dtype=jnp.float32) / RET_DK)
    ang = pos[:, None] * inv_freq[None, :]
    cos, sin = jnp.cos(ang), jnp.sin(ang)
    for layer in range(DEPTH):
        i = layer // 2
        if layer % 2 == 0:
            h = h + _mixer_ab(_rmsnorm(h, norm_ab_w[i]), w_in_ab[i], ret_norm_w[i],
                              s5_lam_re[i], s5_lam_im[i], s5_log_dt[i], s5_b_re[i], s5_b_im[i],
                              s5_c_re[i], s5_c_im[i], s5_d[i], s5_w_glu[i], w_out_ab[i], cos, sin)
        else:
            h = h + _mixer_c(_rmsnorm(h, norm_c_w[i]), w_in_c[i], gla_w_gate[i], gla_b_gate[i],
                             gla_norm_w[i], w_out_c[i])
    return _rmsnorm(h, final_norm_w)[:, N_META:]
```

```python
import math
from contextlib import ExitStack

import numpy as np
import concourse.bass as bass
import concourse.mybir as mybir
from concourse.bass_utils import run_bass_kernel_spmd

F32 = mybir.dt.float32
BF16 = mybir.dt.bfloat16
I32 = mybir.dt.int32
AF = mybir.ActivationFunctionType
ALU = mybir.AluOpType

NCH = 9
NT = NCH * 128
EPS = 1e-6
TWO_PI = 2.0 * math.pi
ENABLE_S5 = True
DEBUG_STOP = None
MAX_OPS = None


class _Stop(Exception):
    pass


_ST = {"stop": False}


def ck(name):
    if DEBUG_STOP == name:
        _ST["stop"] = True
    return _ST["stop"]


class Res:
    __slots__ = ("name", "w", "r", "excl")

    def __init__(self, name, excl=False):
        self.name = name
        self.w = None
        self.r = {}
        self.excl = excl


class Fw:
    def __init__(self, nc, stack):
        self.nc = nc
        self.stack = stack
        self.engs = {}
        for nm in ("pe", "dve", "act", "pool", "sp"):
            sem = stack.enter_context(nc.semaphore("s_" + nm))
            self.engs[nm] = dict(sem=sem, cnt=0, seen={})
        self.dma_sems = {}
        self.q = {k: [] for k in self.engs}
        self.n_ops = 0
        self.log = []

    def _wait(self, en, deps):
        E = self.engs[en]
        best = {}
        for d in deps:
            if d is None:
                continue
            sem, val = d
            if sem is E["sem"] and en in ("pe", "sp"):
                continue
            k = id(sem)
            if E["seen"].get(k, 0) >= val:
                continue
            if k not in best or best[k][1] < val:
                best[k] = (sem, val)
        for k, (sem, val) in best.items():
            self.q[en].append(("w", sem, val))
            E["seen"][k] = val

    @staticmethod
    def _deps(reads, writes):
        deps = []
        for r in reads:
            deps.append(r.w)
        for w in writes:
            deps.append(w.w)
            deps.extend(w.r.values())
        return deps

    @staticmethod
    def _mark(tok, reads, writes):
        for r in reads:
            r.r[id(tok[0])] = tok
        for w in writes:
            w.w = tok
            w.r = {}

    def op(self, en, fn, reads=(), writes=()):
        E = self.engs[en]
        self.n_ops += 1
        if MAX_OPS is not None and self.n_ops > MAX_OPS:
            return None
        import traceback
        self.log.append((self.n_ops, en, traceback.extract_stack(limit=3)[0].lineno, traceback.extract_stack(limit=2)[0].lineno))
        writes = list(writes) + [r for r in reads if r.excl]
        reads = [r for r in reads if not r.excl]
        self._wait(en, self._deps(reads, writes))
        E["cnt"] += 1
        self.q[en].append(("o", fn, E["sem"], 1))
        tok = (E["sem"], E["cnt"])
        self._mark(tok, reads, writes)
        return tok

    def dma(self, en, semname, out, in_, reads=(), writes=()):
        self.n_ops += 1
        if MAX_OPS is not None and self.n_ops > MAX_OPS:
            return None
        import traceback
        self.log.append((self.n_ops, "dma:" + en, traceback.extract_stack(limit=3)[0].lineno, traceback.extract_stack(limit=2)[0].lineno))
        if semname not in self.dma_sems:
            self.dma_sems[semname] = [self.stack.enter_context(self.nc.semaphore("d_" + semname)), 0]
        S = self.dma_sems[semname]
        self._wait(en, self._deps(reads, writes))
        S[1] += 16
        self.q[en].append(("o", lambda e: e.dma_start(out=out, in_=in_), S[0], 16))
        tok = (S[0], S[1])
        self._mark(tok, reads, writes)
        return tok

    def wait_tok(self, en, tok):
        self._wait(en, [tok])

    def barrier(self):
        toks = [(E["sem"], E["cnt"]) for E in self.engs.values() if E["cnt"]]
        toks += [(S[0], S[1]) for S in self.dma_sems.values() if S[1]]
        for en in self.engs:
            self._wait(en, toks)

    def emit(self, block):
        def run(en):
            def body(e):
                for it in self.q[en]:
                    if it[0] == "w":
                        e.wait_ge(it[1], it[2])
                    else:
                        it[1](e).then_inc(it[2], it[3])
            return body
        block.sync(run("sp"))
        block.tensor(run("pe"))
        block.vector(run("dve"))
        block.scalar(run("act"))
        block.gpsimd(run("pool"))


def build_program():
    nc = bass.Bass("TRN2", target_bir_lowering=False)
    _ST["stop"] = False

    def DI(name, shape):
        return nc.dram_tensor(name, list(shape), F32, kind="ExternalInput").ap()

    def DO(name, shape):
        return nc.dram_tensor(name, list(shape), F32, kind="ExternalOutput").ap()

    xin = DI("xin", [NT, 2048])
    w_in_ab = DI("w_in_ab", [2048, 8192])
    w_out_ab = DI("w_out_ab", [3072, 2048])
    w_glu = DI("w_glu", [1024, 1024])
    w_in_c = DI("w_in_c", [2048, 6160])
    w_out_c = DI("w_out_c", [2048, 2048])
    nw_ab = DI("nw_ab", [128, 2048])
    nw_ret = DI("nw_ret", [128, 2048])
    nw_c = DI("nw_c", [128, 2048])
    nw_gla = DI("nw_gla", [128, 2048])
    nw_fin = DI("nw_fin", [128, 2048])
    ident_d = DI("ident", [128, 128])
    maskT_d = DI("maskT", [128, 128])
    triN_d = DI("triN", [128, 128])
    tab_d = DI("ropetab", [128, 8, NCH, 256])
    prem_d = DI("prem", [128, 1])
    wgb_d = DI("wgb", [17, 1024])
    S0_prev = DI("S0_prev", [3, 8, 128, 256])
    S1_prev = DI("S1_prev", [3, 8, 128, 512])
    D1_prev = DI("D1_prev", [3, 128, 8])
    s5_lre = DI("s5_lre", [128, 32])
    s5_lim = DI("s5_lim", [128, 32])
    s5_ldt = DI("s5_ldt", [128, 32])
    s5_bre = DI("s5_bre", [128, 32, 128])
    s5_bim = DI("s5_bim", [128, 32, 128])
    s5_cre = DI("s5_cre", [128, 32, 128])
    s5_cim = DI("s5_cim", [128, 32, 128])
    s5_dd = DI("s5_dd", [128, 8])
    iota_d = DI("iota", [128, 512])
    X5_prev = DI("X5_prev", [3, 2, 128, 32])

    out = DO("out", [1024, 2048])
    h1 = DO("h1", [NT, 2048])
    h2 = DO("h2", [NT, 2048])
    S0_end = DO("S0_end", [8, 128, 256])
    S1_end = DO("S1_end", [8, 128, 512])
    D1_end = DO("D1_end", [128, 8])
    X5_end = DO("X5_end", [2, 128, 32])

    with ExitStack() as st:
        fw = Fw(nc, st)

        def SB(name, shape, dt, stack=st):
            return stack.enter_context(nc.sbuf_tensor(name, list(shape), dt))

        def PS(name, shape, dt, stack=st):
            return stack.enter_context(nc.psum_tensor(name, list(shape), dt))

        out_toks = []
        hnT = SB("hnT", [128, 16, NT], BF16)
        oT = SB("oT", [128, 24, NT], BF16)
        wsl = [SB("wsl0", [128, 8192], BF16), SB("wsl1", [128, 8192], BF16)]
        wn = SB("wn", [128, 2048], F32)
        xts = [SB("xt0", [128, 2048], F32), SB("xt1", [128, 2048], F32)]
        hnb = SB("hnb", [128, 2048], BF16)
        idf = SB("idf", [128, 128], F32)
        idb = SB("idb", [128, 128], BF16)
        maskT = SB("maskT_s", [128, 128], F32)
        triN = SB("triN_s", [128, 128], F32)
        small = SB("small", [128, 16], F32)
        R_hnT = [Res("hnT%d" % c) for c in range(NCH)]
        R_oT = [[Res("oT%d_%d" % (k, c)) for c in range(NCH)] for k in range(24)]
        R_w = [Res("w0"), Res("w1")]
        R_wn = Res("wn")
        R_xt = [Res("xt0"), Res("xt1")]
        R_hnb = Res("hnb")
        R_c = Res("consts")
        R_small = Res("small")
        R_h1 = [Res("h1_%d" % c) for c in range(NCH)]
        R_h2 = [Res("h2_%d" % c) for c in range(NCH)]

        pA = PS("pA", [128, 512], F32)
        pB = PS("pB", [128, 512], F32)
        pT = PS("pT", [128, 1024], BF16)
        pS = PS("pS", [128, 512], F32)
        pO = PS("pO", [128, 512], F32)
        pK = PS("pK", [128, 512], F32)
        pX = PS("pX", [128, 512], F32)
        R_pA, R_pB, R_pT, R_pS, R_pO, R_pK, R_pX = (Res(n, True) for n in ("pA", "pB", "pT", "pS", "pO", "pK", "pX"))

        fw.dma("sp", "c", idf[:], ident_d[:, :], writes=[R_c])
        fw.dma("sp", "c", maskT[:], maskT_d[:, :], writes=[R_c])
        fw.dma("sp", "c", triN[:], triN_d[:, :], writes=[R_c])
        fw.op("dve", lambda e: e.tensor_copy(out=idb[:], in_=idf[:]), reads=[R_c], writes=[R_c])

        wstate = {"n": 0}

        def load_w(dram, kc, ncols, pieces):
            s = wstate["n"] % 2
            wstate["n"] += 1
            dst = wsl[s][:, 0:kc * ncols].rearrange("p (k n) -> p k n", n=ncols)
            src = dram.rearrange("(k p) n -> p k n", p=128)
            for (s0, n, d0) in pieces:
                fw.dma("pool", "w%d" % s, dst[:, :, d0:d0 + n], src[:, :, s0:s0 + n], writes=[R_w[s]])
            return dst, R_w[s]

        cp_state = {"n": 0}

        def evac(out_ap, in_ap, reads, writes):
            cp_state["n"] += 1
            if cp_state["n"] % 2:
                return fw.op("dve", lambda e: e.tensor_copy(out=out_ap, in_=in_ap), reads=reads, writes=writes)
            return fw.op("act", lambda e: e.activation(out=out_ap, in_=in_ap, func=AF.Copy), reads=reads, writes=writes)

        def rstd_from_ss(ss_ap, n, R_extra=()):
            fw.op("dve", lambda e: e.tensor_scalar(out=ss_ap, in0=ss_ap, scalar1=1.0 / n, scalar2=EPS, op0=ALU.mult, op1=ALU.add),
                  reads=[R_small], writes=[R_small])
            fw.op("act", lambda e: e.activation(out=ss_ap, in_=ss_ap, func=AF.Sqrt), reads=[R_small], writes=[R_small])
            fw.op("dve", lambda e: e.reciprocal(out=ss_ap, in_=ss_ap), reads=[R_small], writes=[R_small])

        def transposes_to(src_tile, ncol_tiles, dst_fn, R_src, R_dst_fn):
            for g0 in range(0, ncol_tiles, 8):
                g1 = min(g0 + 8, ncol_tiles)

                def tr(e, g0=g0, g1=g1):
                    last = None
                    for k in range(g0, g1):
                        last = e.transpose(out=pT[:, (k - g0) * 128:(k - g0 + 1) * 128], in_=src_tile[:, k * 128:(k + 1) * 128], identity=idb[:])
                    return last
                fw.op("pe", tr, reads=[R_src, R_c], writes=[R_pT])
                for k in range(g0, g1):
                    evac(dst_fn(k), pT[:, (k - g0) * 128:(k - g0 + 1) * 128], [R_pT], [R_dst_fn(k)])

        def norm_phase(src, R_src, nw_dram, mode):
            fw.dma("sp", "wn", wn[:], nw_dram[:, :], writes=[R_wn])
            for c in range(NCH):
                if mode == "out" and c == 0:
                    continue
                xt = xts[c % 2]
                Rx = R_xt[c % 2]
                fw.dma("sp", "xt%d" % (c % 2), xt[:], src[c * 128:(c + 1) * 128, :], reads=[R_src[c]] if R_src else [], writes=[Rx])
                ss = small[:, 0:1]
                fw.op("dve", lambda e: e.memset(ss, 0.0), writes=[R_small])
                if mode == "T":
                    junk = hnb
                    Rj = R_hnb
                else:
                    junk = None
                if mode == "T":
                    fw.op("act", lambda e, xt=xt: e.activation(out=hnb[:], in_=xt[:], func=AF.Square, accum_out=ss), reads=[Rx], writes=[R_hnb, R_small])
                else:
                    o32 = xts[(c + 1) % 2]
                    fw.op("act", lambda e, xt=xt: e.activation(out=hnb[:], in_=xt[:], func=AF.Square, accum_out=ss), reads=[Rx], writes=[R_hnb, R_small])
                rstd_from_ss(ss, 2048)
                if mode == "T":
                    fw.op("dve", lambda e, xt=xt: e.scalar_tensor_tensor(out=hnb[:], in0=xt[:], scalar=ss, in1=wn[:], op0=ALU.mult, op1=ALU.mult),
                          reads=[Rx, R_small, R_wn], writes=[R_hnb])
                    transposes_to(hnb, 16, lambda k, c=c: hnT[:, k, c * 128:(c + 1) * 128], R_hnb, lambda k, c=c: R_hnT[c])
                else:
                    fw.op("dve", lambda e, xt=xt: e.scalar_tensor_tensor(out=xt[:], in0=xt[:], scalar=ss, in1=wn[:], op0=ALU.mult, op1=ALU.mult),
                          reads=[Rx, R_small, R_wn], writes=[Rx])
                    out_toks.append(fw.dma("sp", "out%d" % (c % 2), out[(c - 1) * 128:c * 128, :], xt[:], reads=[Rx]))


        def out_proj(w_dram, nk, ncols, res_src, R_res_src, dst, R_dst):
            nunits = 2048 // ncols
            for j in range(nunits):
                wv, Rw = load_w(w_dram, nk, ncols, [(j * ncols, ncols, 0)])
                for c in range(NCH):
                    pp, Rp = (pA, R_pA) if (c % 2 == 0) else (pB, R_pB)

                    def mm(e, c=c, pp=pp, wv=wv):
                        last = None
                        for k in range(nk):
                            last = e.matmul(out=pp[:, 0:ncols], lhsT=oT[:, k, c * 128:(c + 1) * 128], rhs=wv[:, k, :], start=(k == 0), stop=(k == nk - 1))
                        return last
                    fw.op("pe", mm, reads=[Rw] + [R_oT[k][c] for k in range(nk)], writes=[Rp])
                    xt = xts[c % 2]
                    Rx = R_xt[c % 2]
                    fw.dma("sp", "xt%d" % (c % 2), xt[:, 0:ncols], res_src[c * 128:(c + 1) * 128, j * ncols:(j + 1) * ncols],
                           reads=[R_res_src[c]] if R_res_src else [], writes=[Rx])
                    fw.op("dve", lambda e, xt=xt, pp=pp: e.tensor_tensor(out=xt[:, 0:ncols], in0=pp[:, 0:ncols], in1=xt[:, 0:ncols], op=ALU.add),
                          reads=[Rp, Rx], writes=[Rx])
                    fw.dma("sp", "hst%d" % (c % 2), dst[c * 128:(c + 1) * 128, j * ncols:(j + 1) * ncols], xt[:, 0:ncols], reads=[Rx], writes=[R_dst[c]])

        def head_norm_gate(po, R_po, ncols, gate_ap, R_gate, on_ap, R_on):
            ss = small[:, 1:2]
            fw.op("dve", lambda e: e.memset(ss, 0.0), writes=[R_small])
            fw.op("act", lambda e: e.activation(out=hnb[:, 0:ncols], in_=po, func=AF.Square, accum_out=ss), reads=[R_po], writes=[R_hnb, R_small])
            rstd_from_ss(ss, ncols)
            fw.op("dve", lambda e: e.scalar_tensor_tensor(out=on_ap, in0=po, scalar=ss, in1=gate_ap, op0=ALU.mult, op1=ALU.mult),
                  reads=[R_po, R_small, R_gate], writes=[R_on])

        try:
            norm_phase(xin, None, nw_ab, "T")
            if ck("norm0"):
                raise _Stop()

            with ExitStack() as ph:
                tab = SB("tab", [128, NCH, 256], F32, ph)
                qk_tm = SB("qk_tm", [128, NCH, 256], BF16, ph)
                qkT = SB("qkT", [128, NCH, 256], BF16, ph)
                v_tm = SB("v_tm", [128, NCH, 256], BF16, ph)
                gate = SB("gate", [128, NCH, 256], F32, ph)
                Sst = SB("Sst", [128, 256], F32, ph)
                Sbf = SB("Sbf", [128, 256], BF16, ph)
                Sin = SB("Sin", [128, 3, 256], F32, ph)
                sTb = SB("sTb", [128, 128], BF16, ph)
                onb = SB("onb", [128, 256], BF16, ph)
                rt = [SB("rt%d" % i, [128, 128], F32, ph) for i in range(4)]
                R_tab, R_qk, R_qkT, R_v, R_gate, R_S, R_Sbf, R_Sin, R_sT, R_on, R_rt = (Res(n) for n in
                    ("tab", "qk", "qkT", "v", "gate", "S", "Sbf", "Sin", "sT", "on", "rt"))
                fw.dma("sp", "wn", wn[:], nw_ret[:, :], writes=[R_wn])
                for h in range(8):
                    g_h = 1.0 - 2.0 ** (-5.0 - h)
                    G = g_h ** 128
                    wA, RwA = load_w(w_in_ab, 16, 512, [(128 * h, 128, 0), (1024 + 128 * h, 128, 128), (2048 + 256 * h, 256, 256)])
                    wB, RwB = load_w(w_in_ab, 16, 256, [(4096 + 256 * h, 256, 0)])
                    fw.dma("sp", "tab", tab[:], tab_d[:, h, :, :], writes=[R_tab])
                    fw.dma("sp", "sin", Sin[:], S0_prev[:, h, :, :].rearrange("s p n -> p s n"), writes=[R_Sin])
                    for c in range(NCH):
                        def mmA(e, c=c, wA=wA):
                            last = None
                            for k in range(16):
                                last = e.matmul(out=pA[:, :], lhsT=hnT[:, k, c * 128:(c + 1) * 128], rhs=wA[:, k, :], start=(k == 0), stop=(k == 15))
                            return last
                        fw.op("pe", mmA, reads=[RwA, R_hnT[c]], writes=[R_pA])

                        def mmB(e, c=c, wB=wB):
                            last = None
                            for k in range(16):
                                last = e.matmul(out=pB[:, 0:256], lhsT=hnT[:, k, c * 128:(c + 1) * 128], rhs=wB[:, k, :], start=(k == 0), stop=(k == 15))
                            return last
                        fw.op("pe", mmB, reads=[RwB, R_hnT[c]], writes=[R_pB])
                        pv = pA[:, 0:256].rearrange("p (a b f) -> p a b f", a=2, b=2)
                        t1 = pv[:, :, 0, :]
                        t2 = pv[:, :, 1, :]
                        Cc = tab[:, c, 0:128].rearrange("p (a f) -> p a f", a=2)
                        Ss = tab[:, c, 128:256].rearrange("p (a f) -> p a f", a=2)
                        r0, r1, r2, r3 = (r[:].rearrange("p (a f) -> p a f", a=2) for r in rt)
                        ov = qk_tm[:, c, :].rearrange("p (a b f) -> p a b f", a=2, b=2)
                        fw.op("dve", lambda e, t1=t1, Cc=Cc, r0=r0: e.tensor_tensor(out=r0, in0=t1, in1=Cc, op=ALU.mult), reads=[R_pA, R_tab], writes=[R_rt])
                        fw.op("dve", lambda e, t2=t2, Ss=Ss, r1=r1: e.tensor_tensor(out=r1, in0=t2, in1=Ss, op=ALU.mult), reads=[R_pA, R_tab], writes=[R_rt])
                        fw.op("dve", lambda e, t1=t1, Ss=Ss, r2=r2: e.tensor_tensor(out=r2, in0=t1, in1=Ss, op=ALU.mult), reads=[R_pA, R_tab], writes=[R_rt])
                        fw.op("dve", lambda e, t2=t2, Cc=Cc, r3=r3: e.tensor_tensor(out=r3, in0=t2, in1=Cc, op=ALU.mult), reads=[R_pA, R_tab], writes=[R_rt])
                        fw.op("dve", lambda e, ov=ov, r0=r0, r1=r1: e.tensor_tensor(out=ov[:, :, 0, :], in0=r0, in1=r1, op=ALU.subtract), reads=[R_rt], writes=[R_qk])
                        fw.op("dve", lambda e, ov=ov, r2=r2, r3=r3: e.tensor_tensor(out=ov[:, :, 1, :], in0=r2, in1=r3, op=ALU.add), reads=[R_rt], writes=[R_qk])
                        fw.op("act", lambda e, c=c: e.activation(out=v_tm[:, c, :], in_=pA[:, 256:512], func=AF.Copy), reads=[R_pA], writes=[R_v])
                        fw.op("act", lambda e, c=c: e.activation(out=gate[:, c, :], in_=pB[:, 0:256], func=AF.Silu), reads=[R_pB], writes=[R_gate])
                        fw.op("dve", lambda e, c=c, h=h: e.tensor_tensor(out=gate[:, c, :], in0=gate[:, c, :], in1=wn[:, 256 * h:256 * (h + 1)], op=ALU.mult),
                              reads=[R_gate, R_wn], writes=[R_gate])

                        def trqk(e, c=c):
                            e.transpose(out=pT[:, 0:128], in_=qk_tm[:, c, 0:128], identity=idb[:])
                            return e.transpose(out=pT[:, 128:256], in_=qk_tm[:, c, 128:256], identity=idb[:])
                        fw.op("pe", trqk, reads=[R_qk, R_c], writes=[R_pT])
                        evac(qkT[:, c, :], pT[:, 0:256], [R_pT], [R_qkT])
                    Gs = G ** 8
                    fw.op("dve", lambda e, Gs=Gs: e.scalar_tensor_tensor(out=Sin[:, 1, :], in0=Sin[:, 0, :], scalar=Gs, in1=Sin[:, 1, :], op0=ALU.mult, op1=ALU.add),
                          reads=[R_Sin], writes=[R_Sin])
                    fw.op("dve", lambda e, Gs=Gs: e.scalar_tensor_tensor(out=Sin[:, 2, :], in0=Sin[:, 1, :], scalar=Gs, in1=Sin[:, 2, :], op0=ALU.mult, op1=ALU.add),
                          reads=[R_Sin], writes=[R_Sin])
                    fw.op("dve", lambda e: e.memset(Sst[:], 0.0), writes=[R_S])
                    for c in range(NCH):
                        if c == 1:
                            fw.op("dve", lambda e: e.tensor_tensor(out=Sst[:], in0=Sst[:], in1=Sin[:, 2, :], op=ALU.add), reads=[R_S, R_Sin], writes=[R_S])
                        if c >= 1:
                            fw.op("act", lambda e: e.activation(out=Sbf[:], in_=Sst[:], func=AF.Copy), reads=[R_S], writes=[R_Sbf])
                        fw.op("pe", lambda e, c=c: e.matmul(out=pS[:, 0:128], lhsT=qkT[:, c, 128:256], rhs=qkT[:, c, 0:128], start=True, stop=True),
                              reads=[R_qkT], writes=[R_pS])
                        fw.op("dve", lambda e: e.tensor_tensor(out=sTb[:], in0=pS[:, 0:128], in1=maskT[:], op=ALU.mult), reads=[R_pS, R_c], writes=[R_sT])

                        def mmo(e, c=c):
                            last = e.matmul(out=pO[:, 0:256], lhsT=sTb[:], rhs=v_tm[:, c, :], start=True, stop=(c == 0))
                            if c >= 1:
                                last = e.matmul(out=pO[:, 0:256], lhsT=qkT[:, c, 0:128], rhs=Sbf[:], start=False, stop=True)
                            return last
                        fw.op("pe", mmo, reads=[R_sT, R_v, R_qkT, R_Sbf], writes=[R_pO])
                        fw.op("pe", lambda e, c=c: e.matmul(out=pK[:, 0:256], lhsT=qk_tm[:, c, 128:256], rhs=v_tm[:, c, :], start=True, stop=True),
                              reads=[R_qk, R_v], writes=[R_pK])
                        fw.op("dve", lambda e: e.tensor_tensor(out=Sst[:], in0=pK[:, 0:256], in1=Sst[:], op=ALU.add), reads=[R_pK, R_S], writes=[R_S])
                        fw.op("act", lambda e, G=G: e.activation(out=Sst[:], in_=Sst[:], func=AF.Copy, scale=G), reads=[R_S], writes=[R_S])
                        head_norm_gate(pO[:, 0:256], R_pO, 256, gate[:, c, :], R_gate, onb[:], R_on)
                        transposes_to(onb, 2, lambda k, c=c, h=h: oT[:, 2 * h + k, c * 128:(c + 1) * 128], R_on, lambda k, c=c, h=h: R_oT[2 * h + k][c])
                    out_toks.append(fw.dma("sp", "sout", S0_end[h, :, :], Sst[:], reads=[R_S]))
                    if ck("ret%d" % h):
                        break

            fw.barrier()
            if _ST["stop"]:
                raise _Stop()
            if not ENABLE_S5:
                for k in range(16, 24):
                    fw.op("dve", lambda e, k=k: e.memset(oT[:, k, :], 0.0), writes=[R_oT[k][c] for c in range(NCH)])
                fw.op("dve", lambda e: e.memset(xts[0][:, 0:64], 0.0), writes=[R_xt[0]])
                out_toks.append(fw.dma("sp", "sout", X5_end.rearrange("a p n -> p a n"), xts[0][:, 0:64].rearrange("p (a n) -> p a n", a=2), reads=[R_xt[0]]))
            else:
                s5_branch(nc, fw, st, locals())

            fw.barrier()
            if ck("mix0"):
                raise _Stop()
            out_proj(w_out_ab, 24, 256, xin, None, h1, R_h1)
            if ck("l0"):
                raise _Stop()

            norm_phase(h1, R_h1, nw_c, "T")
            if ck("norm1"):
                raise _Stop()
            with ExitStack() as ph:
                gl1 = SB("gl1", [32, NT], F32, ph)
                wgb = SB("wgb_s", [32, 1024], F32, ph)
                prem = SB("prem_s", [128, 1], F32, ph)
                qTr = SB("qTr", [128, 2, NT], BF16, ph)
                kTr = SB("kTr", [128, 2, NT], BF16, ph)
                v2 = SB("v2", [128, 512], BF16, ph)
                gate2 = SB("gate2", [128, 512], F32, ph)
                sp_t = SB("sp_t", [128, 256], F32, ph)
                Ep = SB("Ep", [128, 256], F32, ph)
                Em = SB("Em", [128, 256], F32, ph)
                qh = SB("qh", [128, 256], BF16, ph)
                kh = SB("kh", [128, 256], BF16, ph)
                kh_tm = SB("kh_tm", [128, 256], BF16, ph)
                S2 = SB("S2", [128, 2, 512], F32, ph)
                S2b = SB("S2b", [128, 2, 512], BF16, ph)
                S2in = SB("S2in", [128, 3, 2, 512], F32, ph)
                D2in = SB("D2in", [128, 3, 8], F32, ph)
                Dacc = SB("Dacc", [128, 8], F32, ph)
                sT2 = SB("sT2", [128, 128], BF16, ph)
                on2 = SB("on2", [128, 512], BF16, ph)
                (R_gl, R_wgb, R_qTr, R_kTr, R_v2, R_g2, R_sp, R_E, R_qh, R_kh, R_khtm, R_S2, R_S2b, R_S2in, R_D, R_sT2, R_on2) = (
                    Res(n) for n in ("gl", "wgb", "qTr", "kTr", "v2", "g2", "sp", "E", "qh", "kh", "khtm", "S2", "S2b", "S2in", "D", "sT2", "on2"))
                R_prem = Res("prem")
                fw.dma("sp", "wn", wn[:], nw_gla[:, :], writes=[R_wn])
                fw.dma("sp", "c2a", wgb[0:17, :], wgb_d[:, :], writes=[R_wgb])
                fw.dma("sp", "c2b", prem[:], prem_d[:, :], writes=[R_prem])
                fw.dma("sp", "c2c", D2in[:], D1_prev.rearrange("s p n -> p s n"), writes=[R_D])
                fw.op("dve", lambda e: e.memset(gl1[:], 1.0), writes=[R_gl])
                fw.op("dve", lambda e: e.memset(Dacc[:], 1.0), writes=[R_D])
                wG, RwG = load_w(w_in_c, 16, 16, [(6144, 16, 0)])
                for pc in range(3):
                    c0 = pc * 512
                    n = min(512, NT - c0)

                    def mmg(e, c0=c0, n=n, wG=wG):
                        last = None
                        for k in range(16):
                            last = e.matmul(out=pS[0:16, 0:n], lhsT=wG[:, k, :], rhs=hnT[:, k, c0:c0 + n], start=(k == 0), stop=(k == 15))
                        return last
                    fw.op("pe", mmg, reads=[RwG] + R_hnT, writes=[R_pS])
                    fw.op("dve", lambda e, c0=c0, n=n: e.tensor_copy(out=gl1[0:16, c0:c0 + n], in_=pS[0:16, 0:n]), reads=[R_pS], writes=[R_gl])
                for h in range(4):
                    wQK, RwQK = load_w(w_in_c, 16, 512, [(256 * h, 256, 0), (1024 + 256 * h, 256, 256)])
                    for m in range(4):
                        dstT, Rd = (qTr, R_qTr) if m < 2 else (kTr, R_kTr)
                        for pc in range(3):
                            c0 = pc * 512
                            n = min(512, NT - c0)
                            pp, Rp = (pA, R_pA) if ((m * 3 + pc) % 2 == 0) else (pB, R_pB)

                            def mmq(e, m=m, c0=c0, n=n, pp=pp, wQK=wQK):
                                last = None
                                for k in range(16):
                                    last = e.matmul(out=pp[:, 0:n], lhsT=wQK[:, k, m * 128:(m + 1) * 128], rhs=hnT[:, k, c0:c0 + n], start=(k == 0), stop=(k == 15))
                                return last
                            fw.op("pe", mmq, reads=[RwQK] + R_hnT, writes=[Rp])
                            evac(dstT[:, m % 2, c0:c0 + n], pp[:, 0:n], [Rp], [Rd])
                    wV, RwV = load_w(w_in_c, 16, 512, [(2048 + 512 * h, 512, 0)])
                    wZ, RwZ = load_w(w_in_c, 16, 512, [(4096 + 512 * h, 512, 0)])
                    for dc in range(2):
                        col = 2 * h + dc
                        fw.dma("sp", "sin2", S2in[:, :, dc, :], S1_prev[:, 2 * h + dc, :, :].rearrange("s p n -> p s n"), writes=[R_S2in])
                    for dc in range(2):
                        col = 2 * h + dc
                        for s in (1, 2):
                            fw.op("dve", lambda e, s=s, dc=dc, col=col: e.scalar_tensor_tensor(out=S2in[:, s, dc, :], in0=S2in[:, s - 1, dc, :], scalar=D2in[:, s, col:col + 1],
                                                                                              in1=S2in[:, s, dc, :], op0=ALU.mult, op1=ALU.add),
                                  reads=[R_S2in, R_D], writes=[R_S2in])
                    fw.op("dve", lambda e: e.memset(S2[:], 0.0), writes=[R_S2])
                    for c in range(NCH):
                        ts = slice(c * 128, (c + 1) * 128)

                        def mmv(e, c=c, wV=wV):
                            last = None
                            for k in range(16):
                                last = e.matmul(out=pA[:, :], lhsT=hnT[:, k, c * 128:(c + 1) * 128], rhs=wV[:, k, :], start=(k == 0), stop=(k == 15))
                            return last
                        fw.op("pe", mmv, reads=[RwV, R_hnT[c]], writes=[R_pA])
                        evac(v2[:], pA[:, :], [R_pA], [R_v2])

                        def mmz(e, c=c, wZ=wZ):
                            last = None
                            for k in range(16):
                                last = e.matmul(out=pA[:, :], lhsT=hnT[:, k, c * 128:(c + 1) * 128], rhs=wZ[:, k, :], start=(k == 0), stop=(k == 15))
                            return last
                        fw.op("pe", mmz, reads=[RwZ, R_hnT[c]], writes=[R_pA])
                        fw.op("act", lambda e: e.activation(out=gate2[:], in_=pA[:, :], func=AF.Silu), reads=[R_pA], writes=[R_g2])
                        fw.op("dve", lambda e, h=h: e.tensor_tensor(out=gate2[:], in0=gate2[:], in1=wn[:, 512 * h:512 * (h + 1)], op=ALU.mult),
                              reads=[R_g2, R_wn], writes=[R_g2])
                        fw.op("pe", lambda e, ts=ts, h=h: e.matmul(out=pS[:, 0:256], lhsT=gl1[0:17, ts], rhs=wgb[0:17, 256 * h:256 * (h + 1)], start=True, stop=True),
                              reads=[R_gl, R_wgb], writes=[R_pS])
                        fw.op("act", lambda e: e.activation(out=sp_t[:], in_=pS[:, 0:256], func=AF.Exp, scale=-1.0), reads=[R_pS], writes=[R_sp])
                        fw.op("act", lambda e: e.activation(out=sp_t[:], in_=sp_t[:], func=AF.Ln, bias=1.0), reads=[R_sp], writes=[R_sp])
                        if c == 0:
                            fw.op("dve", lambda e: e.tensor_scalar(out=sp_t[:], in0=sp_t[:], scalar1=prem[:, 0:1], scalar2=None, op0=ALU.mult), reads=[R_sp, R_prem], writes=[R_sp])

                        def mmb(e):
                            e.matmul(out=pX[:, 0:128], lhsT=sp_t[:, 0:128], rhs=triN[:], start=True, stop=True)
                            return e.matmul(out=pX[:, 128:256], lhsT=sp_t[:, 128:256], rhs=triN[:], start=True, stop=True)
                        fw.op("pe", mmb, reads=[R_sp, R_c], writes=[R_pX])
                        fw.op("act", lambda e: e.activation(out=Ep[:], in_=pX[:, 0:256], func=AF.Exp), reads=[R_pX], writes=[R_E])
                        fw.op("act", lambda e: e.activation(out=Em[:], in_=pX[:, 0:256], func=AF.Exp, scale=-1.0), reads=[R_pX], writes=[R_E])
                        for dc in range(2):
                            fw.op("dve", lambda e, dc=dc, ts=ts: e.scalar_tensor_tensor(out=qh[:, dc * 128:(dc + 1) * 128], in0=qTr[:, dc, ts], scalar=1.0 / 16.0,
                                                                                        in1=Ep[:, dc * 128:(dc + 1) * 128], op0=ALU.mult, op1=ALU.mult),
                                  reads=[R_qTr, R_E], writes=[R_qh])
                            fw.op("dve", lambda e, dc=dc, ts=ts: e.tensor_tensor(out=kh[:, dc * 128:(dc + 1) * 128], in0=kTr[:, dc, ts], in1=Em[:, dc * 128:(dc + 1) * 128], op=ALU.mult),
                                  reads=[R_kTr, R_E], writes=[R_kh])
                        transposes_to(kh, 2, lambda k: kh_tm[:, k * 128:(k + 1) * 128], R_kh, lambda k: R_khtm)
                        if c == 1:
                            fw.op("dve", lambda e: e.tensor_tensor(out=S2[:], in0=S2[:], in1=S2in[:, 2, :, :], op=ALU.add), reads=[R_S2, R_S2in], writes=[R_S2])
                        if c >= 1:
                            fw.op("act", lambda e: e.activation(out=S2b[:], in_=S2[:], func=AF.Copy), reads=[R_S2], writes=[R_S2b])

                        def mms(e):
                            e.matmul(out=pS[:, 0:128], lhsT=kh[:, 0:128], rhs=qh[:, 0:128], start=True, stop=False)
                            return e.matmul(out=pS[:, 0:128], lhsT=kh[:, 128:256], rhs=qh[:, 128:256], start=False, stop=True)
                        fw.op("pe", mms, reads=[R_kh, R_qh], writes=[R_pS])
                        fw.op("dve", lambda e: e.tensor_tensor(out=sT2[:], in0=pS[:, 0:128], in1=maskT[:], op=ALU.mult), reads=[R_pS, R_c], writes=[R_sT2])

                        def mmo2(e, c=c):
                            last = e.matmul(out=pO[:, :], lhsT=sT2[:], rhs=v2[:], start=True, stop=(c == 0))
                            if c >= 1:
                                e.matmul(out=pO[:, :], lhsT=qh[:, 0:128], rhs=S2b[:, 0, :], start=False, stop=False)
                                last = e.matmul(out=pO[:, :], lhsT=qh[:, 128:256], rhs=S2b[:, 1, :], start=False, stop=True)
                            return last
                        fw.op("pe", mmo2, reads=[R_sT2, R_v2, R_qh, R_S2b], writes=[R_pO])
                        for dc, (pk, Rpk) in enumerate(((pK, R_pK), (pB, R_pB))):
                            fw.op("pe", lambda e, dc=dc, pk=pk, c=c: e.matmul(out=pk[:, :], lhsT=kh_tm[:, dc * 128:(dc + 1) * 128], rhs=v2[:], start=True, stop=True),
                                  reads=[R_khtm, R_v2], writes=[Rpk])
                            fw.op("dve", lambda e, dc=dc, pk=pk: e.tensor_tensor(out=S2[:, dc, :], in0=pk[:, :], in1=S2[:, dc, :], op=ALU.add), reads=[Rpk, R_S2], writes=[R_S2])
                            fw.op("dve", lambda e, dc=dc: e.tensor_scalar(out=S2[:, dc, :], in0=S2[:, dc, :], scalar1=Ep[:, dc * 128 + 127:dc * 128 + 128], scalar2=None, op0=ALU.mult),
                                  reads=[R_S2, R_E], writes=[R_S2])
                            if c >= 1:
                                col = 2 * h + dc
                                fw.op("dve", lambda e, dc=dc, col=col: e.tensor_tensor(out=Dacc[:, col:col + 1], in0=Dacc[:, col:col + 1],
                                                                                       in1=Ep[:, dc * 128 + 127:dc * 128 + 128], op=ALU.mult),
                                      reads=[R_D, R_E], writes=[R_D])
                        head_norm_gate(pO[:, :], R_pO, 512, gate2[:], R_g2, on2[:], R_on2)
                        transposes_to(on2, 4, lambda k, c=c, h=h: oT[:, 4 * h + k, c * 128:(c + 1) * 128], R_on2, lambda k, c=c, h=h: R_oT[4 * h + k][c])
                    for dc in range(2):
                        out_toks.append(fw.dma("sp", "sout", S1_end[2 * h + dc, :, :], S2[:, dc, :], reads=[R_S2]))
                out_toks.append(fw.dma("sp", "sout", D1_end[:, :], Dacc[:], reads=[R_D]))

            fw.barrier()
            if ck("mix1"):
                raise _Stop()
            out_proj(w_out_c, 16, 512, h1, R_h1, h2, R_h2)
            if ck("l1"):
                raise _Stop()
            norm_phase(h2, R_h2, nw_fin, "out")

        except _Stop:
            pass
        for en in ("pe", "dve", "act", "pool"):
            E = fw.engs[en]
            if E["cnt"]:
                fw.wait_tok("sp", (E["sem"], E["cnt"]))
        for nm, S in fw.dma_sems.items():
            fw.wait_tok("sp", (S[0], S[1]))
        block = st.enter_context(nc.Block())
        fw.emit(block)
        _ST["log"] = fw.log
    return nc


class _View:
    def __init__(self, tile, off, width):
        self.tile, self.off, self.width = tile, off, width

    def __getitem__(self, key):
        if isinstance(key, slice):
            return self.tile[:, self.off:self.off + self.width]
        _, cs = key
        a = cs.start or 0
        b = self.width if cs.stop is None else cs.stop
        return self.tile[:, self.off + a:self.off + b]


def s5_branch(nc, fw, st, L):
    hnT, oT, R_hnT, R_oT = L["hnT"], L["oT"], L["R_hnT"], L["R_oT"]
    load_w, evac, SB0, out_toks = L["load_w"], L["evac"], L["SB"], L["out_toks"]
    pA, pB, pS, pO, pK, pX = L["pA"], L["pB"], L["pS"], L["pO"], L["pK"], L["pX"]
    R_pA, R_pB, R_pS, R_pO, R_pK, R_pX = L["R_pA"], L["R_pB"], L["R_pS"], L["R_pO"], L["R_pK"], L["R_pX"]
    w_in_ab, w_glu = L["w_in_ab"], L["w_glu"]
    with ExitStack() as ph:
        def SB(name, shape, dt):
            return SB0(name, shape, dt, ph)
        uT = SB("uT", [128, 8, NT], BF16)
        R_uT = [Res("uT%d" % i) for i in range(8)]
        prm = SB("s5prm", [128, 24, 32], F32)
        prmi = SB("s5prmi", [128, 32], I32)
        R_prm = Res("prm")
        iota = SB("s5iota", [128, 512], F32)
        dd = SB("s5dd", [128, 8], F32)
        xprev = SB("s5xprev", [128, 3, 2, 32], F32)
        xend = SB("s5xend", [128, 2, 32], F32)
        R_cst = Res("s5cst")
        R_xend = Res("xend")
        Bre_u = SB("Bre_u", [128, 4, 128], BF16)
        Bim_u = SB("Bim_u", [128, 4, 128], BF16)
        Cst_re = SB("Cst_re", [128, 4, 128], F32)
        Cst_im = SB("Cst_im", [128, 4, 128], F32)
        Cp_re = SB("Cp_re", [128, 4, 128], BF16)
        Cp_im = SB("Cp_im", [128, 4, 128], BF16)
        ctmp = SB("ctmp", [128, 128], F32)
        R_B, R_Cst, R_Cp, R_ctmp = Res("B"), Res("Cst"), Res("Cp"), Res("ctmp")
        cosT = SB("cosT", [128, 512], F32)
        sinT = SB("sinT", [128, 512], F32)
        xts = L["xts"]
        t1, t2, btr, bti = (_View(xts[0], 512 * i, 512) for i in range(4))
        xtr, xti, tw, rdec = (_View(xts[1], 512 * i, 512) for i in range(4))
        twi = SB("twi", [128, 512], I32)
        R_tab, R_tw = Res("s5tab"), Res("tw")
        xre = SB("xre", [128, 512], BF16)
        xim = SB("xim", [128, 512], BF16)
        st5 = SB("st5", [128, 8], F32)
        R_t, R_bt, R_xt5, R_x, R_st = Res("t12"), Res("bt"), Res("xt5"), Res("x"), Res("st5")

        P = lambda i: prm[:, i, :]
        LRE, LIM, LDT, DT, AA, MAG, THF, SN, CS, ABR, ABI, DEN, NR, FRE, FIM, NFIM, RSR, RSI, XR, XI, TA, TB, TC, TD = range(24)

        def dve(fn, reads=(R_prm,), writes=(R_prm,)):
            return fw.op("dve", fn, reads=list(reads), writes=list(writes))

        def act(fn, reads=(R_prm,), writes=(R_prm,)):
            return fw.op("act", fn, reads=list(reads), writes=list(writes))

        def frac(dst, src):
            dve(lambda e: e.tensor_copy(out=prmi[:], in_=src))
            dve(lambda e: e.tensor_copy(out=P(TD), in_=prmi[:]))
            dve(lambda e: e.tensor_tensor(out=dst, in0=src, in1=P(TD), op=ALU.subtract))

        fw.dma("sp", "s5a", prm[:, LRE, :], L["s5_lre"][:, :], writes=[R_prm])
        fw.dma("sp", "s5b", prm[:, LIM, :], L["s5_lim"][:, :], writes=[R_prm])
        fw.dma("sp", "s5c", prm[:, LDT, :], L["s5_ldt"][:, :], writes=[R_prm])
        fw.dma("sp", "s5d", iota[:], L["iota_d"][:, :], writes=[R_cst])
        fw.dma("sp", "s5e", dd[:], L["s5_dd"][:, :], writes=[R_cst])
        fw.dma("sp", "s5f", xprev[:].rearrange("p s a n -> p (s a) n"), L["X5_prev"].rearrange("s a p n -> p (s a) n"), writes=[R_prm])
        act(lambda e: e.activation(out=P(DT), in_=P(LDT), func=AF.Exp))
        dve(lambda e: e.tensor_tensor(out=P(AA), in0=P(LRE), in1=P(DT), op=ALU.mult))
        act(lambda e: e.activation(out=P(MAG), in_=P(AA), func=AF.Exp))
        dve(lambda e: e.tensor_tensor(out=P(TA), in0=P(LIM), in1=P(DT), op=ALU.mult))
        dve(lambda e: e.tensor_scalar(out=P(TA), in0=P(TA), scalar1=1.0 / TWO_PI, scalar2=None, op0=ALU.mult))
        frac(P(THF), P(TA))
        act(lambda e: e.activation(out=P(SN), in_=P(THF), func=AF.Sin, scale=TWO_PI))
        dve(lambda e: e.tensor_scalar(out=P(TA), in0=P(THF), scalar1=0.25, scalar2=None, op0=ALU.add))
        frac(P(TB), P(TA))
        act(lambda e: e.activation(out=P(CS), in_=P(TB), func=AF.Sin, scale=TWO_PI))
        dve(lambda e: e.tensor_tensor(out=P(ABR), in0=P(MAG), in1=P(CS), op=ALU.mult))
        dve(lambda e: e.tensor_tensor(out=P(ABI), in0=P(MAG), in1=P(SN), op=ALU.mult))
        dve(lambda e: e.tensor_tensor(out=P(DEN), in0=P(LRE), in1=P(LRE), op=ALU.mult))
        dve(lambda e: e.tensor_tensor(out=P(TA), in0=P(LIM), in1=P(LIM), op=ALU.mult))
        dve(lambda e: e.tensor_tensor(out=P(DEN), in0=P(DEN), in1=P(TA), op=ALU.add))
        dve(lambda e: e.reciprocal(out=P(DEN), in_=P(DEN)))
        dve(lambda e: e.tensor_scalar(out=P(NR), in0=P(ABR), scalar1=-1.0, scalar2=None, op0=ALU.add))
        dve(lambda e: e.tensor_tensor(out=P(TA), in0=P(NR), in1=P(LRE), op=ALU.mult))
        dve(lambda e: e.tensor_tensor(out=P(TB), in0=P(ABI), in1=P(LIM), op=ALU.mult))
        dve(lambda e: e.tensor_tensor(out=P(TA), in0=P(TA), in1=P(TB), op=ALU.add))
        dve(lambda e: e.tensor_tensor(out=P(FRE), in0=P(TA), in1=P(DEN), op=ALU.mult))
        dve(lambda e: e.tensor_tensor(out=P(TA), in0=P(ABI), in1=P(LRE), op=ALU.mult))
        dve(lambda e: e.tensor_tensor(out=P(TB), in0=P(NR), in1=P(LIM), op=ALU.mult))
        dve(lambda e: e.tensor_tensor(out=P(TA), in0=P(TA), in1=P(TB), op=ALU.subtract))
        dve(lambda e: e.tensor_tensor(out=P(FIM), in0=P(TA), in1=P(DEN), op=ALU.mult))
        dve(lambda e: e.tensor_scalar(out=P(NFIM), in0=P(FIM), scalar1=-1.0, scalar2=None, op0=ALU.mult))
        act(lambda e: e.activation(out=P(TC), in_=P(AA), func=AF.Exp, scale=1024.0))
        dve(lambda e: e.tensor_scalar(out=P(TA), in0=P(THF), scalar1=1024.0, scalar2=None, op0=ALU.mult))
        frac(P(TB), P(TA))
        act(lambda e: e.activation(out=P(RSI), in_=P(TB), func=AF.Sin, scale=TWO_PI))
        dve(lambda e: e.tensor_scalar(out=P(TA), in0=P(TB), scalar1=0.25, scalar2=None, op0=ALU.add))
        frac(P(TB), P(TA))
        act(lambda e: e.activation(out=P(RSR), in_=P(TB), func=AF.Sin, scale=TWO_PI))
        dve(lambda e: e.tensor_tensor(out=P(RSR), in0=P(RSR), in1=P(TC), op=ALU.mult))
        dve(lambda e: e.tensor_tensor(out=P(RSI), in0=P(RSI), in1=P(TC), op=ALU.mult))
        dve(lambda e: e.tensor_copy(out=P(XR), in_=xprev[:, 0, 0, :]))
        dve(lambda e: e.tensor_copy(out=P(XI), in_=xprev[:, 0, 1, :]))
        for sl in (1, 2):
            dve(lambda e: e.tensor_tensor(out=P(TA), in0=P(XR), in1=P(RSR), op=ALU.mult))
            dve(lambda e: e.tensor_tensor(out=P(TB), in0=P(XI), in1=P(RSI), op=ALU.mult))
            dve(lambda e: e.tensor_tensor(out=P(TA), in0=P(TA), in1=P(TB), op=ALU.subtract))
            dve(lambda e: e.tensor_tensor(out=P(TB), in0=P(XR), in1=P(RSI), op=ALU.mult))
            dve(lambda e: e.tensor_tensor(out=P(TC), in0=P(XI), in1=P(RSR), op=ALU.mult))
            dve(lambda e: e.tensor_tensor(out=P(TB), in0=P(TB), in1=P(TC), op=ALU.add))
            dve(lambda e, sl=sl: e.tensor_tensor(out=P(XR), in0=P(TA), in1=xprev[:, sl, 0, :], op=ALU.add))
            dve(lambda e, sl=sl: e.tensor_tensor(out=P(XI), in0=P(TB), in1=xprev[:, sl, 1, :], op=ALU.add))

        pieces3 = [(0, 512), (512, 512), (1024, 128)]
        for q in range(2):
            wU, RwU = load_w(w_in_ab, 16, 512, [(6144 + 512 * q, 512, 0)])
            for m in range(4):
                ut = 4 * q + m
                for pi, (c0, n) in enumerate(pieces3):
                    pp, Rp = (pA, R_pA) if ((m * 3 + pi) % 2 == 0) else (pB, R_pB)

                    def mmu(e, m=m, c0=c0, n=n, pp=pp, wU=wU):
                        last = None
                        for k in range(16):
                            last = e.matmul(out=pp[:, 0:n], lhsT=wU[:, k, m * 128:(m + 1) * 128], rhs=hnT[:, k, c0:c0 + n], start=(k == 0), stop=(k == 15))
                        return last
                    fw.op("pe", mmu, reads=[RwU] + R_hnT, writes=[Rp])
                    evac(uT[:, ut, c0:c0 + n], pp[:, 0:n], [Rp], [R_uT[ut]])

        spieces = [(112, 16, pS, R_pS), (128, 512, pO, R_pO), (640, 512, pK, R_pK)]
        for ut in range(8):
            fw.dma("pool", "s5B", Bre_u[:], L["s5_bre"][:, 4 * ut:4 * ut + 4, :], writes=[R_B])
            fw.dma("pool", "s5B", Bim_u[:], L["s5_bim"][:, 4 * ut:4 * ut + 4, :], writes=[R_B])
            fw.dma("sp", "s5C", Cst_re[:], L["s5_cre"][:, 4 * ut:4 * ut + 4, :], writes=[R_Cst])
            fw.dma("sp", "s5C", Cst_im[:], L["s5_cim"][:, 4 * ut:4 * ut + 4, :], writes=[R_Cst])
            for j in range(4):
                t = 4 * ut + j
                fw.op("dve", lambda e, j=j, t=t: e.tensor_scalar(out=ctmp[:], in0=Cst_im[:, j, :], scalar1=prm[:, FIM, t:t + 1], scalar2=None, op0=ALU.mult),
                      reads=[R_Cst, R_prm], writes=[R_ctmp])
                fw.op("dve", lambda e, j=j, t=t: e.scalar_tensor_tensor(out=Cp_re[:, j, :], in0=Cst_re[:, j, :], scalar=prm[:, FRE, t:t + 1], in1=ctmp[:], op0=ALU.mult, op1=ALU.subtract),
                      reads=[R_Cst, R_prm, R_ctmp], writes=[R_Cp])
                fw.op("dve", lambda e, j=j, t=t: e.tensor_scalar(out=ctmp[:], in0=Cst_im[:, j, :], scalar1=prm[:, FRE, t:t + 1], scalar2=None, op0=ALU.mult),
                      reads=[R_Cst, R_prm], writes=[R_ctmp])
                fw.op("dve", lambda e, j=j, t=t: e.scalar_tensor_tensor(out=Cp_im[:, j, :], in0=Cst_re[:, j, :], scalar=prm[:, NFIM, t:t + 1], in1=ctmp[:], op0=ALU.mult, op1=ALU.subtract),
                      reads=[R_Cst, R_prm, R_ctmp], writes=[R_Cp])
            for j in range(4):
                t = 4 * ut + j
                fw.op("dve", lambda e, t=t: e.tensor_scalar(out=tw[:], in0=iota[:], scalar1=prm[:, THF, t:t + 1], scalar2=None, op0=ALU.mult),
                      reads=[R_cst, R_prm], writes=[R_tw])
                fw.op("dve", lambda e: e.tensor_copy(out=twi[:], in_=tw[:]), reads=[R_tw], writes=[R_tw])
                fw.op("dve", lambda e: e.tensor_copy(out=t1[:], in_=twi[:]), reads=[R_tw], writes=[R_t])
                fw.op("dve", lambda e: e.tensor_tensor(out=tw[:], in0=tw[:], in1=t1[:], op=ALU.subtract), reads=[R_tw, R_t], writes=[R_tw])
                fw.op("act", lambda e: e.activation(out=sinT[:], in_=tw[:], func=AF.Sin, scale=TWO_PI), reads=[R_tw], writes=[R_tab])
                fw.op("dve", lambda e: e.tensor_scalar(out=tw[:], in0=tw[:], scalar1=0.25, scalar2=None, op0=ALU.add), reads=[R_tw], writes=[R_tw])
                fw.op("dve", lambda e: e.tensor_copy(out=twi[:], in_=tw[:]), reads=[R_tw], writes=[R_tw])
                fw.op("dve", lambda e: e.tensor_copy(out=t1[:], in_=twi[:]), reads=[R_tw], writes=[R_t])
                fw.op("dve", lambda e: e.tensor_tensor(out=tw[:], in0=tw[:], in1=t1[:], op=ALU.subtract), reads=[R_tw, R_t], writes=[R_tw])
                fw.op("act", lambda e: e.activation(out=cosT[:], in_=tw[:], func=AF.Sin, scale=TWO_PI), reads=[R_tw], writes=[R_tab])
                fw.op("dve", lambda e, t=t: e.tensor_scalar(out=rdec[:], in0=iota[:], scalar1=0.0, scalar2=prm[:, MAG, t:t + 1], op0=ALU.mult, op1=ALU.add),
                      reads=[R_cst, R_prm], writes=[R_tab])
                for pi, (c0, n, pY, R_pY) in enumerate(spieces):
                    def mmb(e, j=j, c0=c0, n=n, ut=ut):
                        e.matmul(out=pA[:, 0:n], lhsT=Bre_u[:, j, :], rhs=uT[:, ut, c0:c0 + n], start=True, stop=True)
                        return e.matmul(out=pB[:, 0:n], lhsT=Bim_u[:, j, :], rhs=uT[:, ut, c0:c0 + n], start=True, stop=True)
                    fw.op("pe", mmb, reads=[R_B, R_uT[ut]], writes=[R_pA, R_pB])
                    cs, sn = cosT[:, 0:n], sinT[:, 0:n]
                    a1, a2 = t1[:, 0:n], t2[:, 0:n]
                    rd = [R_pA, R_pB, R_tab]
                    fw.op("dve", lambda e, n=n, a1=a1, cs=cs: e.tensor_tensor(out=a1, in0=pA[:, 0:n], in1=cs, op=ALU.mult), reads=rd, writes=[R_t])
                    fw.op("dve", lambda e, n=n, a2=a2, sn=sn: e.tensor_tensor(out=a2, in0=pB[:, 0:n], in1=sn, op=ALU.mult), reads=rd, writes=[R_t])
                    fw.op("dve", lambda e, n=n, a1=a1, a2=a2: e.tensor_tensor(out=btr[:, 0:n], in0=a1, in1=a2, op=ALU.add), reads=[R_t], writes=[R_bt])
                    fw.op("dve", lambda e, n=n, a1=a1, cs=cs: e.tensor_tensor(out=a1, in0=pB[:, 0:n], in1=cs, op=ALU.mult), reads=rd, writes=[R_t])
                    fw.op("dve", lambda e, n=n, a2=a2, sn=sn: e.tensor_tensor(out=a2, in0=pA[:, 0:n], in1=sn, op=ALU.mult), reads=rd, writes=[R_t])
                    fw.op("dve", lambda e, n=n, a1=a1, a2=a2: e.tensor_tensor(out=bti[:, 0:n], in0=a1, in1=a2, op=ALU.subtract), reads=[R_t], writes=[R_bt])
                    ini_r = 0.0 if pi == 0 else st5[:, 2:3]
                    ini_i = 0.0 if pi == 0 else st5[:, 3:4]
                    fw.op("dve", lambda e, n=n, ini_r=ini_r: e.tensor_tensor_scan(out=xtr[:, 0:n], data0=rdec[:, 0:n], data1=btr[:, 0:n], initial=ini_r, op0=ALU.mult, op1=ALU.add),
                          reads=[R_bt, R_tab, R_st], writes=[R_xt5])
                    fw.op("dve", lambda e, n=n, ini_i=ini_i: e.tensor_tensor_scan(out=xti[:, 0:n], data0=rdec[:, 0:n], data1=bti[:, 0:n], initial=ini_i, op0=ALU.mult, op1=ALU.add),
                          reads=[R_bt, R_tab, R_st], writes=[R_xt5])
                    fw.op("dve", lambda e, n=n, a1=a1, cs=cs: e.tensor_tensor(out=a1, in0=xtr[:, 0:n], in1=cs, op=ALU.mult), reads=[R_xt5, R_tab], writes=[R_t])
                    fw.op("dve", lambda e, n=n, a2=a2, sn=sn: e.tensor_tensor(out=a2, in0=xti[:, 0:n], in1=sn, op=ALU.mult), reads=[R_xt5, R_tab], writes=[R_t])
                    fw.op("dve", lambda e, n=n, a1=a1, a2=a2: e.tensor_tensor(out=xre[:, 0:n], in0=a1, in1=a2, op=ALU.subtract), reads=[R_t], writes=[R_x])
                    fw.op("dve", lambda e, n=n: e.tensor_tensor(out=st5[:, 0:1], in0=t1[:, n - 1:n], in1=t2[:, n - 1:n], op=ALU.subtract), reads=[R_t], writes=[R_st])
                    fw.op("dve", lambda e, n=n, a1=a1, sn=sn: e.tensor_tensor(out=a1, in0=xtr[:, 0:n], in1=sn, op=ALU.mult), reads=[R_xt5, R_tab], writes=[R_t])
                    fw.op("dve", lambda e, n=n, a2=a2, cs=cs: e.tensor_tensor(out=a2, in0=xti[:, 0:n], in1=cs, op=ALU.mult), reads=[R_xt5, R_tab], writes=[R_t])
                    fw.op("dve", lambda e, n=n, a1=a1, a2=a2: e.tensor_tensor(out=xim[:, 0:n], in0=a1, in1=a2, op=ALU.add), reads=[R_t], writes=[R_x])
                    fw.op("dve", lambda e, n=n: e.tensor_tensor(out=st5[:, 1:2], in0=t1[:, n - 1:n], in1=t2[:, n - 1:n], op=ALU.add), reads=[R_t], writes=[R_st])
                    if pi == 0:
                        fw.op("dve", lambda e, t=t: e.tensor_tensor(out=st5[:, 0:1], in0=st5[:, 0:1], in1=prm[:, XR, t:t + 1], op=ALU.add), reads=[R_st, R_prm], writes=[R_st])
                        fw.op("dve", lambda e, t=t: e.tensor_tensor(out=st5[:, 1:2], in0=st5[:, 1:2], in1=prm[:, XI, t:t + 1], op=ALU.add), reads=[R_st, R_prm], writes=[R_st])
                    if pi == 2:
                        fw.op("dve", lambda e, t=t: e.tensor_copy(out=xend[:, 0, t:t + 1], in_=st5[:, 0:1]), reads=[R_st], writes=[R_xend])
                        fw.op("dve", lambda e, t=t: e.tensor_copy(out=xend[:, 1, t:t + 1], in_=st5[:, 1:2]), reads=[R_st], writes=[R_xend])
                    else:
                        fw.op("dve", lambda e: e.tensor_tensor(out=st5[:, 4:5], in0=st5[:, 1:2], in1=sinT[:, 1:2], op=ALU.mult), reads=[R_st, R_tab], writes=[R_st])
                        fw.op("dve", lambda e: e.scalar_tensor_tensor(out=st5[:, 2:3], in0=st5[:, 0:1], scalar=cosT[:, 1:2], in1=st5[:, 4:5], op0=ALU.mult, op1=ALU.subtract),
                              reads=[R_st, R_tab], writes=[R_st])
                        fw.op("dve", lambda e: e.tensor_tensor(out=st5[:, 4:5], in0=st5[:, 1:2], in1=cosT[:, 1:2], op=ALU.mult), reads=[R_st, R_tab], writes=[R_st])
                        fw.op("dve", lambda e: e.scalar_tensor_tensor(out=st5[:, 3:4], in0=st5[:, 0:1], scalar=sinT[:, 1:2], in1=st5[:, 4:5], op0=ALU.mult, op1=ALU.add),
                              reads=[R_st, R_tab], writes=[R_st])

                    def mmy(e, j=j, n=n, pY=pY):
                        e.matmul(out=pY[:, 0:n], lhsT=Cp_re[:, j, :], rhs=xre[:, 0:n], start=(j == 0), stop=False)
                        return e.matmul(out=pY[:, 0:n], lhsT=Cp_im[:, j, :], rhs=xim[:, 0:n], start=False, stop=(j == 3))
                    fw.op("pe", mmy, reads=[R_Cp, R_x], writes=[R_pY])
            for (c0, n, pY, R_pY) in spieces:
                ys = btr[:, 0:n]
                fw.op("dve", lambda e, ut=ut, c0=c0, n=n, pY=pY, ys=ys: e.scalar_tensor_tensor(out=ys, in0=uT[:, ut, c0:c0 + n], scalar=dd[:, ut:ut + 1], in1=pY[:, 0:n], op0=ALU.mult, op1=ALU.add),
                      reads=[R_uT[ut], R_cst, R_pY], writes=[R_bt])
                fw.op("dve", lambda e, n=n, ys=ys: e.tensor_tensor(out=t1[:, 0:n], in0=ys, in1=ys, op=ALU.mult), reads=[R_bt], writes=[R_t])
                fw.op("dve", lambda e, n=n: e.tensor_scalar(out=t1[:, 0:n], in0=t1[:, 0:n], scalar1=0.044715, scalar2=1.0, op0=ALU.mult, op1=ALU.add), reads=[R_t], writes=[R_t])
                fw.op("dve", lambda e, n=n, ys=ys: e.tensor_tensor(out=t1[:, 0:n], in0=t1[:, 0:n], in1=ys, op=ALU.mult), reads=[R_t, R_bt], writes=[R_t])
                fw.op("act", lambda e, n=n: e.activation(out=t2[:, 0:n], in_=t1[:, 0:n], func=AF.Sigmoid, scale=1.5957691216057308), reads=[R_t], writes=[R_t])
                fw.op("dve", lambda e, ut=ut, c0=c0, n=n, ys=ys: e.tensor_tensor(out=uT[:, ut, c0:c0 + n], in0=ys, in1=t2[:, 0:n], op=ALU.mult), reads=[R_bt, R_t], writes=[R_uT[ut]])
        out_toks.append(fw.dma("sp", "sout", L["X5_end"].rearrange("a p n -> p a n"), xend[:], reads=[R_xend]))

        pieces_o = [(0, 128), (128, 512), (640, 512)]
        for q in range(2):
            wG, RwG = load_w(w_glu, 8, 512, [(512 * q, 512, 0)])
            wZ, RwZ = load_w(w_in_ab, 16, 512, [(7168 + 512 * q, 512, 0)])
            for m in range(4):
                oc = 4 * q + m
                for (c0, n) in pieces_o:
                    def mmg(e, m=m, c0=c0, n=n, wG=wG):
                        last = None
                        for k in range(8):
                            last = e.matmul(out=pA[:, 0:n], lhsT=wG[:, k, m * 128:(m + 1) * 128], rhs=uT[:, k, c0:c0 + n], start=(k == 0), stop=(k == 7))
                        return last
                    fw.op("pe", mmg, reads=[RwG] + R_uT, writes=[R_pA])

                    def mmz(e, m=m, c0=c0, n=n, wZ=wZ):
                        last = None
                        for k in range(16):
                            last = e.matmul(out=pB[:, 0:n], lhsT=wZ[:, k, m * 128:(m + 1) * 128], rhs=hnT[:, k, c0:c0 + n], start=(k == 0), stop=(k == 15))
                        return last
                    fw.op("pe", mmz, reads=[RwZ] + R_hnT, writes=[R_pB])
                    fw.op("act", lambda e, n=n: e.activation(out=t1[:, 0:n], in_=pA[:, 0:n], func=AF.Sigmoid), reads=[R_pA], writes=[R_t])
                    fw.op("act", lambda e, n=n: e.activation(out=t2[:, 0:n], in_=pB[:, 0:n], func=AF.Silu), reads=[R_pB], writes=[R_t])
                    fw.op("dve", lambda e, n=n, oc=oc, c0=c0: e.tensor_tensor(out=t1[:, 0:n], in0=t1[:, 0:n], in1=uT[:, oc, c0:c0 + n], op=ALU.mult), reads=[R_t, R_uT[oc]], writes=[R_t])
                    cl = sorted(set([c0 // 128 + i for i in range((n + 127) // 128)]))
                    fw.op("dve", lambda e, n=n, oc=oc, c0=c0: e.tensor_tensor(out=oT[:, 16 + oc, c0:c0 + n], in0=t1[:, 0:n], in1=t2[:, 0:n], op=ALU.mult),
                          reads=[R_t], writes=[R_oT[16 + oc][c] for c in cl])


_PROG = {}


def _consts(k):
    c = {}
    c["ident"] = np.eye(128, dtype=np.float32)
    j = np.arange(128)
    c["maskT"] = (j[:, None] <= j[None, :]).astype(np.float32)
    c["triN"] = (c["maskT"] * np.float32(-1.0 / 16.0)).astype(np.float32)
    slot = np.arange(NT)
    row = slot % 128
    pos = np.where(slot < 128, np.maximum(row - 112, 0), 16 + 1024 * k + (slot - 128)).astype(np.float32)
    inv_freq = np.power(np.float32(10000.0), -np.arange(0, 128, 2, dtype=np.float32) / np.float32(128)).astype(np.float32)
    ang = (pos[:, None] * inv_freq[None, :]).astype(np.float32)
    cos, sin = np.cos(ang).astype(np.float64), np.sin(ang).astype(np.float64)
    tab = np.zeros((NT, 8, 256), np.float64)
    for h in range(8):
        g = 1.0 - 2.0 ** (-5.0 - h)
        gq = g ** (row + 1.0)
        gk = g ** (-(row + 1.0)) * (128 ** -0.5)
        tab[:, h, 0:64] = cos * gq[:, None]
        tab[:, h, 64:128] = cos * gk[:, None]
        tab[:, h, 128:192] = sin * gq[:, None]
        tab[:, h, 192:256] = sin * gk[:, None]
    c["ropetab"] = np.ascontiguousarray(tab.reshape(NCH, 128, 8, 256).transpose(1, 2, 0, 3)).astype(np.float32)
    prem = np.zeros((128, 1), np.float32)
    if k == 0:
        prem[112:] = 1.0
    c["prem"] = prem
    c["iota"] = np.ascontiguousarray(np.broadcast_to(np.arange(512, dtype=np.float32), (128, 512)))
    return c


def _pair_cols(a):
    return np.ascontiguousarray(a.reshape(32, 2, 64).transpose(1, 2, 0).reshape(128, 32))


def _prep_inputs(inp):
    f = lambda a: np.ascontiguousarray(np.asarray(a, dtype=np.float32))
    x = f(inp["x"])
    meta = f(inp["meta"])
    bc = lambda v: np.ascontiguousarray(np.broadcast_to(f(v).reshape(1, 2048), (128, 2048)))
    shared = {
        "w_in_ab": f(inp["w_in_ab"])[0], "w_out_ab": f(inp["w_out_ab"])[0], "w_glu": f(inp["s5_w_glu"])[0],
        "w_in_c": f(inp["w_in_c"])[0], "w_out_c": f(inp["w_out_c"])[0],
        "nw_ab": bc(inp["norm_ab_w"]), "nw_ret": bc(inp["ret_norm_w"]), "nw_c": bc(inp["norm_c_w"]),
        "nw_gla": bc(inp["gla_norm_w"]), "nw_fin": bc(inp["final_norm_w"]),
        "wgb": np.ascontiguousarray(np.concatenate([f(inp["gla_w_gate"])[0], f(inp["gla_b_gate"])[0][None, :]], axis=0)),
    }
    shared["s5_lre"] = _pair_cols(f(inp["s5_lam_re"])[0])
    shared["s5_lim"] = _pair_cols(f(inp["s5_lam_im"])[0])
    shared["s5_ldt"] = _pair_cols(np.repeat(f(inp["s5_log_dt"])[0][:, None], 64, axis=1))
    bre, bim = f(inp["s5_b_re"])[0], f(inp["s5_b_im"])[0]
    cre, cim = f(inp["s5_c_re"])[0], f(inp["s5_c_im"])[0]
    Bre = np.zeros((128, 32, 128), np.float32); Bim = np.zeros_like(Bre)
    Cre = np.zeros((128, 32, 128), np.float32); Cim = np.zeros_like(Cre)
    for t in range(32):
        for g2 in range(2):
            g = 2 * t + g2
            g8 = g % 8
            Bre[g8 * 16:(g8 + 1) * 16, t, g2 * 64:(g2 + 1) * 64] = bre[g].T
            Bim[g8 * 16:(g8 + 1) * 16, t, g2 * 64:(g2 + 1) * 64] = bim[g].T
            Cre[g2 * 64:(g2 + 1) * 64, t, g8 * 16:(g8 + 1) * 16] = cre[g].T
            Cim[g2 * 64:(g2 + 1) * 64, t, g8 * 16:(g8 + 1) * 16] = cim[g].T
    shared.update({"s5_bre": Bre, "s5_bim": Bim, "s5_cre": Cre, "s5_cim": Cim})
    shared["s5_dd"] = np.ascontiguousarray(f(inp["s5_d"])[0].reshape(8, 128).T)
    per_core = []
    for c in range(8):
        b, k = c // 4, c % 4
        xin = np.zeros((NT, 2048), np.float32)
        xin[128:] = x[b, 1024 * k:1024 * (k + 1)]
        if k == 0:
            xin[112:128] = meta
        d = dict(shared)
        d.update(_consts(k))
        d["xin"] = xin
        per_core.append(d)
    return per_core


def _prev(states, c, shape):
    b, k = c // 4, c % 4
    slots = [np.zeros(shape, np.float32)] * (3 - k) + [states[b * 4 + kk] for kk in range(k)]
    return np.ascontiguousarray(np.stack(slots, axis=0))


def kernel(**inputs):
    if "nc" not in _PROG:
        _PROG["nc"] = build_program()
    nc = _PROG["nc"]
    per_core = _prep_inputs(inputs)
    z = lambda *s: np.zeros(s, np.float32)
    S0 = [z(8, 128, 256)] * 8
    X5 = [z(2, 128, 32)] * 8
    S1 = [z(8, 128, 512)] * 8
    D1 = [z(128, 8)] * 8
    res = None
    for it in range(3):
        in_maps = []
        for c in range(8):
            d = dict(per_core[c])
            d["S0_prev"] = _prev(S0, c, (8, 128, 256))
            d["X5_prev"] = _prev(X5, c, (2, 128, 32))
            d["S1_prev"] = _prev(S1, c, (8, 128, 512))
            d["D1_prev"] = _prev(D1, c, (128, 8))
            in_maps.append(d)
        res = run_bass_kernel_spmd(nc, in_maps, core_ids=list(range(8))).results
        if it == 0:
            S0 = [r["S0_end"] for r in res]
            X5 = [r["X5_end"] for r in res]
        if it == 1:
            S1 = [r["S1_end"] for r in res]
            D1 = [r["D1_end"] for r in res]
    outp = np.zeros((2, 4096, 2048), np.float32)
    for c in range(8):
        b, k = c // 4, c % 4
        outp[b, 1024 * k:1024 * (k + 1)] = res[c]["out"]
    _PROG["last"] = res
    return outp
```
